# Optimizing a Trainium2 kernel written in Bass

```python
import jax, jax.numpy as jnp
from jax import lax
import numpy as np

D_MODEL = 1024
BATCH = 16
SEQ = 4096
DEPTH = 4
DEC_BATCH = 32
DEC_SEQ = 2048
PAST_LEN = 128

N_MIXERS = 3
N_A_LAYERS = (DEPTH + 2) // 3
N_B_LAYERS = (DEPTH + 1) // 3
N_C_LAYERS = DEPTH // 3
NORM_EPS = 1e-6
ROPE_THETA = 10000.0
NEG_INF = -1e30

A_HEADS = 16
A_KV_HEADS = 4
A_GROUP = A_HEADS // A_KV_HEADS
A_HEAD_DIM = 64
A_WINDOW = 128
A_BLOCK = 128
A_QKV_COLS = (A_HEADS + 2 * A_KV_HEADS) * A_HEAD_DIM

B_HEADS = 16
B_Q_LORA = 384
B_KV_LORA = 256
B_NOPE = 64
B_ROPE = 32
B_V = 64
B_QBLOCK = 128
B_IN_COLS = B_Q_LORA + B_KV_LORA + B_ROPE

GRID_W = 64
C_HEADS = 16
C_HEAD_DIM = 64
C_WIN_ROWS = 8
C_WIN_COLS = 16
C_QROWS = 2
C_KCOLS = 2 * C_WIN_COLS
C_NCB = GRID_W // C_WIN_COLS
C_QKV_COLS = 3 * C_HEADS * C_HEAD_DIM

FFN_DIM = 2816
CONV_WIDTH = 3

kernel_name = "hybrid_swa_mla_natten_convffn_encoder"


def rmsnorm(x, g):
    xf = x.astype(jnp.float32)
    y = xf * lax.rsqrt(jnp.mean(xf * xf, axis=-1, keepdims=True) + NORM_EPS)
    return (y * g.astype(jnp.float32)).astype(x.dtype)


def rope(x, pos):
    d = x.shape[-1]
    inv = ROPE_THETA ** (-jnp.arange(0, d, 2, dtype=jnp.float32) / d)
    ang = pos.astype(jnp.float32)[:, None] * inv[None, :]
    cos, sin = jnp.cos(ang)[:, None, :], jnp.sin(ang)[:, None, :]
    x1 = x[..., : d // 2].astype(jnp.float32)
    x2 = x[..., d // 2:].astype(jnp.float32)
    return jnp.concatenate([x1 * cos - x2 * sin, x1 * sin + x2 * cos], axis=-1).astype(x.dtype)


def window_attention(h, w_qkv, w_o, sink):
    B, T, _ = h.shape
    nb = T // A_BLOCK
    pos = jnp.arange(T)
    qkv = h @ w_qkv
    q = qkv[..., : A_HEADS * A_HEAD_DIM].reshape(B, T, A_HEADS, A_HEAD_DIM)
    k = qkv[..., A_HEADS * A_HEAD_DIM:(A_HEADS + A_KV_HEADS) * A_HEAD_DIM].reshape(B, T, A_KV_HEADS, A_HEAD_DIM)
    v = qkv[..., (A_HEADS + A_KV_HEADS) * A_HEAD_DIM:].reshape(B, T, A_KV_HEADS, A_HEAD_DIM)
    q = rope(q, pos).reshape(B, nb, A_BLOCK, A_KV_HEADS, A_GROUP, A_HEAD_DIM)
    k = rope(k, pos)
    pad = ((0, 0), (A_BLOCK, A_BLOCK), (0, 0), (0, 0))
    kp = jnp.pad(k, pad).reshape(B, nb + 2, A_BLOCK, A_KV_HEADS, A_HEAD_DIM)
    vp = jnp.pad(v, pad).reshape(B, nb + 2, A_BLOCK, A_KV_HEADS, A_HEAD_DIM)
    kb = jnp.concatenate([kp[:, :-2], kp[:, 1:-1], kp[:, 2:]], axis=2)
    vb = jnp.concatenate([vp[:, :-2], vp[:, 1:-1], vp[:, 2:]], axis=2)
    s = jnp.einsum('bnqhgd,bnkhd->bnhgqk', q, kb).astype(jnp.float32) * (A_HEAD_DIM ** -0.5)
    qpos = jnp.arange(nb)[:, None] * A_BLOCK + jnp.arange(A_BLOCK)[None, :]
    kpos = jnp.arange(nb)[:, None] * A_BLOCK - A_BLOCK + jnp.arange(3 * A_BLOCK)[None, :]
    mask = (jnp.abs(qpos[:, :, None] - kpos[:, None, :]) <= A_WINDOW) & ((kpos >= 0) & (kpos < T))[:, None, :]
    s = jnp.where(mask[None, :, None, None], s, NEG_INF)
    sl = sink.astype(jnp.float32).reshape(A_KV_HEADS, A_GROUP)[None, None, :, :, None, None]
    m = jnp.maximum(jnp.max(s, axis=-1, keepdims=True), sl)
    p = jnp.exp(s - m)
    p = p / (jnp.sum(p, axis=-1, keepdims=True) + jnp.exp(sl - m))
    o = jnp.einsum('bnhgqk,bnkhd->bnqhgd', p.astype(vb.dtype), vb)
    return o.reshape(B, T, A_HEADS * A_HEAD_DIM) @ w_o


def latent_attention(h, w_in, q_norm, kv_norm, w_uq, w_ukv, w_o):
    B, T, _ = h.shape
    nqb = T // B_QBLOCK
    pos = jnp.arange(T)
    c = h @ w_in
    cq = rmsnorm(c[..., :B_Q_LORA], q_norm)
    ckv = rmsnorm(c[..., B_Q_LORA:B_Q_LORA + B_KV_LORA], kv_norm)
    kr = rope(c[..., B_Q_LORA + B_KV_LORA:][:, :, None, :], pos)[:, :, 0]
    q = (cq @ w_uq).reshape(B, T, B_HEADS, B_NOPE + B_ROPE)
    q_nope = q[..., :B_NOPE]
    q_rope = rope(q[..., B_NOPE:], pos)
    kv = (ckv @ w_ukv).reshape(B, T, B_HEADS, B_NOPE + B_V)
    k_nope, v = kv[..., :B_NOPE], kv[..., B_NOPE:]
    scale = (B_NOPE + B_ROPE) ** -0.5
    qn = q_nope.reshape(B, nqb, B_QBLOCK, B_HEADS, B_NOPE).transpose(1, 0, 2, 3, 4)
    qr = q_rope.reshape(B, nqb, B_QBLOCK, B_HEADS, B_ROPE).transpose(1, 0, 2, 3, 4)

    def block(args):
        qn_b, qr_b = args
        s = (jnp.einsum('bqhd,bkhd->bhqk', qn_b, k_nope)
             + jnp.einsum('bqhd,bkd->bhqk', qr_b, kr)).astype(jnp.float32) * scale
        p = jax.nn.softmax(s, axis=-1)
        return jnp.einsum('bhqk,bkhd->bqhd', p.astype(v.dtype), v)

    o = lax.map(block, (qn, qr))
    return o.transpose(1, 0, 2, 3, 4).reshape(B, T, B_HEADS * B_V) @ w_o


def neighbourhood_attention(h, w_qkv, rpb, w_o):
    B, T, _ = h.shape
    rows = T // GRID_W
    wr = min(C_WIN_ROWS, rows)
    kr_n = min(wr + 1, rows)
    nrb = rows // C_QROWS
    qkv = (h @ w_qkv).reshape(B, rows, GRID_W, 3, C_HEADS, C_HEAD_DIM)
    q = qkv[:, :, :, 0] * (C_HEAD_DIM ** -0.5)
    k = qkv[:, :, :, 1]
    v = qkv[:, :, :, 2]
    kc0 = np.clip(np.arange(C_NCB) * C_WIN_COLS - C_WIN_COLS // 2, 0, GRID_W - C_KCOLS)
    kcols = kc0[:, None] + np.arange(C_KCOLS)[None, :]
    qcol = np.arange(GRID_W).reshape(C_NCB, C_WIN_COLS)
    cs = np.clip(qcol - C_WIN_COLS // 2, 0, GRID_W - C_WIN_COLS)
    col_ok = (kcols[:, None, :] >= cs[:, :, None]) & (kcols[:, None, :] < cs[:, :, None] + C_WIN_COLS)
    dci = np.clip(kcols[:, None, :] - qcol[:, :, None] + C_WIN_COLS - 1, 0, 2 * C_WIN_COLS - 2)

    def block(rb):
        r0 = rb * C_QROWS
        kr0 = jnp.clip(r0 - wr // 2, 0, rows - kr_n)
        q_b = lax.dynamic_slice_in_dim(q, r0, C_QROWS, axis=1).reshape(
            B, C_QROWS, C_NCB, C_WIN_COLS, C_HEADS, C_HEAD_DIM)
        k_b = lax.dynamic_slice_in_dim(k, kr0, kr_n, axis=1)[:, :, kcols]
        v_b = lax.dynamic_slice_in_dim(v, kr0, kr_n, axis=1)[:, :, kcols]
        qrow = r0 + jnp.arange(C_QROWS)
        krow = kr0 + jnp.arange(kr_n)
        rs = jnp.clip(qrow - wr // 2, 0, rows - wr)
        row_ok = (krow[None, :] >= rs[:, None]) & (krow[None, :] < rs[:, None] + wr)
        dri = jnp.clip(krow[None, :] - qrow[:, None] + C_WIN_ROWS - 1, 0, 2 * C_WIN_ROWS - 2)
        bias = rpb.astype(jnp.float32)[:, dri[None, :, None, :, None], dci[:, None, :, None, :]]
        mask = row_ok[None, :, None, :, None] & col_ok[:, None, :, None, :]
        s = jnp.einsum('brjchd,bkjlhd->bhjrckl', q_b, k_b).astype(jnp.float32) + bias[None]
        s = jnp.where(mask[None, None], s, NEG_INF)
        sh = s.shape
        p = jax.nn.softmax(s.reshape(sh[:-2] + (sh[-2] * sh[-1],)), axis=-1).reshape(sh)
        o = jnp.einsum('bhjrckl,bkjlhd->brjchd', p.astype(v_b.dtype), v_b)
        return o.reshape(B, C_QROWS, GRID_W, C_HEADS * C_HEAD_DIM)

    o = lax.map(block, jnp.arange(nrb))
    return o.transpose(1, 0, 2, 3, 4).reshape(B, T, C_HEADS * C_HEAD_DIM) @ w_o


def conv_ffn(h, w_in, conv_w, conv_b, w_out):
    T = h.shape[1]
    u = h @ w_in
    up = jnp.pad(u, ((0, 0), (1, 1), (0, 0)))
    u = conv_b + up[:, :T] * conv_w[0] + up[:, 1:T + 1] * conv_w[1] + up[:, 2:T + 2] * conv_w[2]
    g, val = u[..., :FFN_DIM], u[..., FFN_DIM:]
    return (jax.nn.silu(g) * val) @ w_out


def trunk(x, norm_mix, norm_ffn, norm_final, a_w_qkv, a_w_o, a_sink,
          b_w_in, b_q_norm, b_kv_norm, b_w_uq, b_w_ukv, b_w_o,
          c_w_qkv, c_rpb, c_w_o, f_w_in, f_conv_w, f_conv_b, f_w_out):
    for i in range(DEPTH):
        kind, j = i % N_MIXERS, i // N_MIXERS
        h = rmsnorm(x, norm_mix[i])
        if kind == 0:
            x = x + window_attention(h, a_w_qkv[j], a_w_o[j], a_sink[j])
        elif kind == 1:
            x = x + latent_attention(h, b_w_in[j], b_q_norm[j], b_kv_norm[j], b_w_uq[j], b_w_ukv[j], b_w_o[j])
        else:
            x = x + neighbourhood_attention(h, c_w_qkv[j], c_rpb[j], c_w_o[j])
        x = x + conv_ffn(rmsnorm(x, norm_ffn[i]), f_w_in[i], f_conv_w[i], f_conv_b[i], f_w_out[i])
    return rmsnorm(x, norm_final)


def _dense(key, shape, fan_in):
    return jax.random.normal(key, shape, jnp.float32) * fan_in ** -0.5


def _gain(key, shape):
    return 1.0 + 0.1 * jax.random.normal(key, shape, jnp.float32)


def setup_inputs(seed: int = 0) -> dict:
    key = jax.random.key(seed)
    ks = jax.random.split(key, 21)
    return {
        "x_prompt": jax.random.normal(ks[0], (BATCH, SEQ, D_MODEL), jnp.float32),
        "x_sample": jax.random.normal(ks[1], (DEC_BATCH, DEC_SEQ, D_MODEL), jnp.float32),
        "norm_mix": _gain(ks[2], (DEPTH, D_MODEL)),
        "norm_ffn": _gain(ks[3], (DEPTH, D_MODEL)),
        "norm_final": _gain(ks[4], (D_MODEL,)),
        "a_w_qkv": _dense(ks[5], (N_A_LAYERS, D_MODEL, A_QKV_COLS), D_MODEL),
        "a_w_o": _dense(ks[6], (N_A_LAYERS, A_HEADS * A_HEAD_DIM, D_MODEL), A_HEADS * A_HEAD_DIM),
        "a_sink": 0.5 * jax.random.normal(ks[7], (N_A_LAYERS, A_HEADS), jnp.float32),
        "b_w_in": _dense(ks[8], (N_B_LAYERS, D_MODEL, B_IN_COLS), D_MODEL),
        "b_q_norm": _gain(ks[9], (N_B_LAYERS, B_Q_LORA)),
        "b_kv_norm": _gain(ks[10], (N_B_LAYERS, B_KV_LORA)),
        "b_w_uq": _dense(ks[11], (N_B_LAYERS, B_Q_LORA, B_HEADS * (B_NOPE + B_ROPE)), B_Q_LORA),
        "b_w_ukv": _dense(ks[12], (N_B_LAYERS, B_KV_LORA, B_HEADS * (B_NOPE + B_V)), B_KV_LORA),
        "b_w_o": _dense(ks[13], (N_B_LAYERS, B_HEADS * B_V, D_MODEL), B_HEADS * B_V),
        "c_w_qkv": _dense(ks[14], (N_C_LAYERS, D_MODEL, C_QKV_COLS), D_MODEL),
        "c_rpb": 0.5 * jax.random.normal(ks[15], (N_C_LAYERS, C_HEADS, 2 * C_WIN_ROWS - 1, 2 * C_WIN_COLS - 1), jnp.float32),
        "c_w_o": _dense(ks[16], (N_C_LAYERS, C_HEADS * C_HEAD_DIM, D_MODEL), C_HEADS * C_HEAD_DIM),
        "f_w_in": _dense(ks[17], (DEPTH, D_MODEL, 2 * FFN_DIM), D_MODEL),
        "f_conv_w": _dense(ks[18], (DEPTH, CONV_WIDTH, 2 * FFN_DIM), CONV_WIDTH),
        "f_conv_b": 0.01 * jax.random.normal(ks[19], (DEPTH, 2 * FFN_DIM), jnp.float32),
        "f_w_out": _dense(ks[20], (DEPTH, FFN_DIM, D_MODEL), FFN_DIM),
    }


def reference(x_prompt, x_sample, norm_mix, norm_ffn, norm_final, a_w_qkv, a_w_o, a_sink,
              b_w_in, b_q_norm, b_kv_norm, b_w_uq, b_w_ukv, b_w_o,
              c_w_qkv, c_rpb, c_w_o, f_w_in, f_conv_w, f_conv_b, f_w_out):
    y_prompt = trunk(x_prompt, norm_mix, norm_ffn, norm_final, a_w_qkv, a_w_o, a_sink,
                     b_w_in, b_q_norm, b_kv_norm, b_w_uq, b_w_ukv, b_w_o,
                     c_w_qkv, c_rpb, c_w_o, f_w_in, f_conv_w, f_conv_b, f_w_out)
    y_sample = trunk(x_sample, norm_mix, norm_ffn, norm_final, a_w_qkv, a_w_o, a_sink,
                     b_w_in, b_q_norm, b_kv_norm, b_w_uq, b_w_ukv, b_w_o,
                     c_w_qkv, c_rpb, c_w_o, f_w_in, f_conv_w, f_conv_b, f_w_out)
    return (y_prompt, y_sample)
```

```python
import numpy as np
from contextlib import ExitStack
import concourse.bass as bass
import concourse.mybir as mybir
from concourse.bass_utils import run_bass_kernel_spmd

F32 = mybir.dt.float32
BF16 = mybir.dt.bfloat16
ALU = mybir.AluOpType
AF = mybir.ActivationFunctionType

D = 1024
KC = 8
FF = 2816
NF = 22
EPS = 1e-6
THETA = 10000.0
NCORES = 8


class Sem:
    def __init__(self, h):
        self.h = h
        self.cnt = 0


class Eng:
    def __init__(self, K, e, name, nq=0):
        self.K = K
        self.e = e
        self.name = name
        self.sem = K.newsem(name)
        self.seen = {}
        self.qs = [K.newsem(f"{name}q{i}") for i in range(nq)]
        self.qi = 0

    def wait(self, evs):
        for sem, val in evs:
            if self.seen.get(sem, 0) < val:
                self.e.wait_ge(sem.h, val)
                self.seen[sem] = val

    def mark(self, ins):
        self.sem.cnt += 1
        ins.then_inc(self.sem.h, 1)
        return (self.sem, self.sem.cnt)

    def dma_raw(self, out, in_):
        q = self.qs[self.qi % len(self.qs)]
        self.qi += 1
        if q.cnt:
            self.wait([(q, q.cnt)])
        ins = self.e.dma_start(out=out, in_=in_)
        q.cnt += 16
        ins.then_inc(q.h, 16)
        return (q, q.cnt)


class Buf:
    def __init__(self, ap=None):
        self.ap = ap
        self.w = {}
        self.r = {}
        self.pr = {}


def _merge(d, ev):
    sem, v = ev
    if d.get(sem, 0) < v:
        d[sem] = v


class Kern:
    def __init__(self, nc):
        self.nc = nc
        self.es = ExitStack()
        self.nsem = 0
        self.pe = Eng(self, nc.tensor, "pe")
        self.act = Eng(self, nc.scalar, "act")
        self.dve = Eng(self, nc.vector, "dve")
        self.pool = Eng(self, nc.gpsimd, "pool", nq=32)
        self.sp = Eng(self, nc.sync, "sp", nq=48)
        self.engs = [self.pe, self.act, self.dve, self.pool, self.sp]
        self.bar = self.newsem("bar")
        self.uid = 0

    def newsem(self, name):
        self.nsem += 1
        return Sem(self.es.enter_context(self.nc.semaphore(name)))

    def name(self, p):
        self.uid += 1
        return f"{p}{self.uid}"

    def begin(self, b):
        pr = dict(b.w)
        for sv in b.r.items():
            _merge(pr, sv)
        b.pr = pr
        b.w = {}
        b.r = {}

    def deps(self, reads, writes, pw=()):
        d = []
        for b in reads:
            d += list(b.w.items())
        for b in writes:
            d += list(b.w.items()) + list(b.r.items())
        for b in pw:
            d += list(b.pr.items())
        return d

    def commit(self, ev, reads, writes, pw=()):
        for b in reads:
            _merge(b.r, ev)
        for b in writes:
            b.w = {ev[0]: ev[1]}
            b.r = {}
        for b in pw:
            _merge(b.w, ev)

    def op(self, eng, fn, reads=(), writes=(), pw=()):
        eng.wait(self.deps(reads, writes, pw))
        ev = eng.mark(fn())
        self.commit(ev, reads, writes, pw)
        return ev

    def dma(self, eng, out, in_, reads=(), writes=(), pw=()):
        eng.wait(self.deps(reads, writes, pw))
        ev = eng.dma_raw(out, in_)
        self.commit(ev, reads, writes, pw)
        return ev

    def mm(self, out_buf, out_ap, items, first=True, last=True):
        PE = self.pe
        if first:
            PE.wait(self.deps((), (out_buf,)))
        n = len(items)
        ins = None
        for i, (l, r, rb) in enumerate(items):
            PE.wait(self.deps(rb, ()))
            ins = PE.e.matmul(out_ap, lhsT=l, rhs=r, start=(i == 0), stop=(i == n - 1))
        ev = PE.mark(ins)
        for (_, _, rb) in items:
            for b in rb:
                _merge(b.r, ev)
        if last:
            out_buf.w = {ev[0]: ev[1]}
            out_buf.r = {}
        return ev

    def barrier(self):
        for E in self.engs:
            evs = [(q, q.cnt) for q in E.qs if q.cnt]
            if E.sem.cnt:
                evs.append((E.sem, E.sem.cnt))
            E.wait(evs)
            E.e.sem_inc(self.bar.h, 1)
        self.bar.cnt += len(self.engs)
        for E in self.engs:
            E.e.wait_ge(self.bar.h, self.bar.cnt)


class Scope:
    def __init__(self, K):
        self.K = K
        self.es = ExitStack()

    def sb(self, shape, dt, name="t"):
        return self.es.enter_context(self.K.nc.sbuf_tensor(self.K.name(name), list(shape), dt))

    def ps(self, shape, dt=F32, name="p"):
        return self.es.enter_context(self.K.nc.psum_tensor(self.K.name(name), list(shape), dt))

    def close(self):
        self.es.close()


class Ring:
    def __init__(self, tensors):
        self.t = tensors
        self.b = [Buf(t) for t in tensors]
        self.i = 0

    def next(self):
        j = self.i % len(self.t)
        self.i += 1
        return self.t[j], self.b[j]


def stat_layout(W):
    Kd, M = W.shape
    return np.ascontiguousarray(W.reshape(Kd // 128, 128, M // 128, 128).transpose(2, 1, 0, 3))


def mov_layout(W):
    Kd, N = W.shape
    return np.ascontiguousarray(W.reshape(Kd // 128, 128, N).transpose(1, 0, 2))


def head_swap_idx(nheads, hd):
    idx = []
    for h in range(nheads):
        idx += list(range(h * hd + hd // 2, h * hd + hd)) + list(range(h * hd, h * hd + hd // 2))
    return np.array(idx)


def rope_tables(tmax):
    pos = np.arange(tmax).astype(np.float32)
    inv64 = (THETA ** (-np.arange(0, 64, 2, dtype=np.float32) / 64)).astype(np.float32)
    ang = pos[None, :] * inv64[:, None]
    c64 = np.cos(ang).astype(np.float32)
    s64 = np.sin(ang).astype(np.float32)
    C64 = np.concatenate([c64, c64, c64, c64], 0)
    S64 = np.concatenate([-s64, s64, -s64, s64], 0)
    inv32 = (THETA ** (-np.arange(0, 32, 2, dtype=np.float32) / 32)).astype(np.float32)
    ang = pos[None, :] * inv32[:, None]
    c32 = np.cos(ang).astype(np.float32)
    s32 = np.sin(ang).astype(np.float32)
    C32 = np.zeros((128, tmax), np.float32)
    S32 = np.zeros((128, tmax), np.float32)
    C32[64:96] = np.concatenate([c32, c32], 0)
    S32[64:96] = np.concatenate([-s32, s32], 0)
    return np.stack([C64, S64, C32, S32], 0)


def c_block_types(R):
    out = []
    nb = R // 2
    for rb in range(nb):
        if rb < 2:
            out.append((1 + rb, 0, 4))
        elif rb >= nb - 2:
            out.append((3 + (rb - (nb - 2)), nb - 4, 4))
        else:
            out.append((0, rb - 2, 5))
    return out


C_TOFF = [0, 5, 9, 13, 17]


def c_tables_idx(R=32):
    dri = np.zeros((21, 128, 128), np.int64)
    dci = np.zeros((21, 128, 128), np.int64)
    val = np.zeros((21, 128, 128), np.float32)
    types = c_block_types(R)
    rep = {}
    for rb, (ty, kb0, nk) in enumerate(types):
        if ty not in rep:
            rep[ty] = (rb, kb0, nk)
    kl = np.arange(128)
    krl, kc = kl // 64, kl % 64
    ql = np.arange(128)
    qrl, qc = ql // 64, ql % 64
    for ty, (rb, kb0, nk) in rep.items():
        for i in range(nk):
            krow = (2 * (kb0 + i) + krl)[:, None]
            qrow = (2 * rb + qrl)[None, :]
            rs = np.clip(qrow - 4, 0, R - 8)
            cs = np.clip(qc - 8, 0, 48)[None, :]
            ok = (krow >= rs) & (krow < rs + 8) & (kc[:, None] >= cs) & (kc[:, None] < cs + 16)
            t = C_TOFF[ty] + i
            dri[t] = np.clip(krow - qrow + 7, 0, 14)
            dci[t] = np.clip(kc[:, None] - qc[None, :] + 15, 0, 30)
            val[t] = ok.astype(np.float32)
    return dri, dci, val


def prep_shared(inp):
    f = lambda a: np.asarray(a, dtype=np.float32)
    sh = {}
    sw64 = head_swap_idx(16, 64)
    sw64k = head_swap_idx(4, 64)
    a_qkv = f(inp["a_w_qkv"])
    a_st, a_mv = [], []
    for j in range(a_qkv.shape[0]):
        Wq, Wk, Wv = a_qkv[j][:, :1024], a_qkv[j][:, 1024:1280], a_qkv[j][:, 1280:1536]
        a_st.append(stat_layout(np.concatenate([Wq, Wq[:, sw64], Wk, Wk[:, sw64k]], 1)))
        a_mv.append(mov_layout(Wv))
    sh["a_st"] = np.stack(a_st)
    sh["a_mv"] = np.stack(a_mv)
    sh["a_wo"] = np.stack([stat_layout(w) for w in f(inp["a_w_o"])])
    sh["a_sink"] = f(inp["a_sink"])
    b_in = f(inp["b_w_in"])[0]
    z = lambda n: np.zeros((b_in.shape[0], n), np.float32)
    kr = b_in[:, 640:672]
    kr_sw = np.concatenate([kr[:, 16:], kr[:, :16]], 1)
    sh["b_in"] = stat_layout(np.concatenate([b_in[:, :640], z(64), kr, z(32), z(64), kr_sw, z(32)], 1))[None]
    uq = f(inp["b_w_uq"])[0]
    zq = lambda n: np.zeros((384, n), np.float32)
    cols = []
    for h in range(16):
        cols += [uq[:, 96 * h:96 * h + 96], zq(32)]
    for h in range(16):
        r = uq[:, 96 * h + 64:96 * h + 96]
        cols += [zq(64), np.concatenate([r[:, 16:], r[:, :16]], 1), zq(32)]
    sh["b_uq"] = stat_layout(np.concatenate(cols, 1))[None]
    ukv = f(inp["b_w_ukv"])[0]
    zk = np.zeros((256, 64), np.float32)
    cols = []
    for h in range(16):
        cols += [ukv[:, 128 * h:128 * h + 64], zk]
    sh["b_uk"] = stat_layout(np.concatenate(cols, 1))[None]
    sh["b_uv"] = mov_layout(np.concatenate([ukv[:, 128 * h + 64:128 * h + 128] for h in range(16)], 1))[None]
    sh["b_wo"] = stat_layout(f(inp["b_w_o"])[0])[None]
    sh["b_gn"] = np.concatenate([f(inp["b_q_norm"])[0].reshape(3, 128).T, f(inp["b_kv_norm"])[0].reshape(2, 128).T], 1).copy()
    cq = f(inp["c_w_qkv"])[0]
    sh["c_st"] = stat_layout(cq[:, :2048])[None]
    sh["c_mv"] = mov_layout(cq[:, 2048:])[None]
    sh["c_wo"] = stat_layout(f(inp["c_w_o"])[0])[None]
    dri, dci, val = c_tables_idx()
    rpb = f(inp["c_rpb"])[0]
    tb = rpb[:, dri, dci] * val[None]
    sh["c_bias"] = np.ascontiguousarray(tb.transpose(0, 2, 1, 3).reshape(16, 128, 21 * 128))
    sh["c_valid"] = np.ascontiguousarray(val.transpose(1, 0, 2).reshape(128, 21 * 128))
    sh["f_in"] = np.stack([stat_layout(w) for w in f(inp["f_w_in"])])
    sh["f_out"] = np.stack([stat_layout(w) for w in f(inp["f_w_out"])])
    cw = np.concatenate([f(inp["f_conv_w"]), f(inp["f_conv_b"])[:, None, :]], 1)
    sh["f_cw"] = np.ascontiguousarray(cw.reshape(4, 4, 44, 128).transpose(0, 3, 2, 1))
    g = np.concatenate([f(inp["norm_mix"]), f(inp["norm_ffn"]), f(inp["norm_final"])[None]], 0)
    sh["gains"] = np.ascontiguousarray(g.reshape(9, 8, 128).transpose(2, 0, 1))
    sh["rope"] = rope_tables(4096)
    kk = np.arange(128)[:, None]
    qq = np.arange(128)[None, :]
    sh["a_mask"] = np.concatenate([(qq <= kk), np.ones((128, 128), bool), (kk <= qq)], 1).astype(np.float32)
    sh["ident"] = np.eye(128, dtype=np.float32)
    return sh


WEIGHT_BF16 = ["a_st", "a_mv", "a_wo", "b_in", "b_uq", "b_uk", "b_uv", "b_wo", "c_st", "c_mv", "c_wo", "f_in", "f_out", "a_mask"]


def ffn_windows(T):
    nw = -(-T // 510)
    s = -(-T // nw)
    wins = []
    o = 0
    while o < T:
        n = min(s, T - o)
        wins.append((o, n))
        o += n
    return wins


class Prog:
    def __init__(self, seqs, sh_shapes, depth=4, dbg=False, do_final=True, skip=()):
        self.seqs = seqs
        self.depth = depth
        self.dbg = dbg
        self.do_final = do_final
        self.skip = set(skip)
        self.Tmax = max(T for _, _, T in seqs)
        nc = bass.Bass("TRN2", target_bir_lowering=False)
        self.nc = nc
        self.K = Kern(nc)
        K = self.K
        npr = max([i + 1 for g, i, T in seqs if g == "p"] + [1])
        nsa = max([i + 1 for g, i, T in seqs if g == "s"] + [1])
        Tp = max([T for g, i, T in seqs if g == "p"] + [128])
        Ts = max([T for g, i, T in seqs if g == "s"] + [128])
        self.xin = {"p": nc.dram_tensor("x_p", [npr, Tp, D], F32, kind="ExternalInput").ap(),
                    "s": nc.dram_tensor("x_s", [nsa, Ts, D], F32, kind="ExternalInput").ap()}
        self.yout = {"p": nc.dram_tensor("y_p", [npr, Tp, D], F32, kind="ExternalOutput").ap(),
                     "s": nc.dram_tensor("y_s", [nsa, Ts, D], F32, kind="ExternalOutput").ap()}
        self.win = {}
        for k, shp in sh_shapes.items():
            self.win[k] = nc.dram_tensor("w_" + k, list(shp), F32, kind="ExternalInput").ap()
        self.wb = {}
        for k in WEIGHT_BF16:
            self.wb[k] = nc.dram_tensor("wb_" + k, list(sh_shapes[k]), BF16).ap()
        Tm = self.Tmax
        kind = "ExternalOutput" if dbg else "Internal"
        self.XT = [nc.dram_tensor(f"XT{i}", [D, Tm], F32, kind=kind).ap() for i in range(2)]
        self.QT = nc.dram_tensor("QT", [1536, Tm], BF16).ap()
        self.KT = nc.dram_tensor("KT", [1536, Tm], BF16).ap()
        self.VS = nc.dram_tensor("VS", [Tm, 1024], BF16).ap()
        self.OT = nc.dram_tensor("OT", [D, Tm], BF16).ap()
        self.CT = nc.dram_tensor("CT", [16, 128, 21 * 128], BF16).ap()
        self.build()

    def build(self):
        K = self.K
        nc = self.nc
        G = Scope(K)
        self.G = G
        self.ident = G.sb([128, 128], F32, "ident")
        self.ones = G.sb([128, 128], BF16, "ones")
        self.gains = G.sb([128, 9, 8], F32, "gains")
        self.bgn = G.sb([128, 5], F32, "bgn")
        self.cB = Buf()
        K.dma(K.sp, self.ident[:], self.win["ident"][:, :], writes=[self.cB])
        K.dma(K.sp, self.gains[:], self.win["gains"][:, :, :], writes=[self.cB])
        K.dma(K.sp, self.bgn[:], self.win["b_gn"][:, :], writes=[self.cB])
        K.op(K.pool, lambda: nc.gpsimd.memset(self.ones[:], 1.0), writes=[self.cB])
        self.epsb = G.sb([128, 1], F32, "eps")
        K.op(K.pool, lambda: nc.gpsimd.memset(self.epsb[:], EPS), writes=[self.cB])
        self.prologue()
        K.barrier()
        for (g, i, T) in self.seqs:
            self.phase_in(g, i, T)
            cur = 0
            for l in range(self.depth):
                kind = l % 3
                j = l // 3
                if kind == 0:
                    'proj' in self.skip or self.phase_proj_a(l, j, T, cur)
                    'att' in self.skip or self.phase_att("a", j, T)
                    wo = self.wb["a_wo"][j]
                elif kind == 1:
                    'proj' in self.skip or self.phase_proj_b(l, j, T, cur)
                    'att' in self.skip or self.phase_att("b", j, T)
                    wo = self.wb["b_wo"][j]
                else:
                    'proj' in self.skip or self.phase_proj_c(l, j, T, cur)
                    'att' in self.skip or self.phase_att("c", j, T)
                    wo = self.wb["c_wo"][j]
                'ffn' in self.skip or self.phase_woffn(l, wo, T, cur)
                cur = 1 - cur
            if self.do_final:
                self.phase_final(g, i, T, cur)
        K.barrier()
        G.close()

    def prologue(self):
        K = self.K
        nc = self.nc
        for k in WEIGHT_BF16:
            src = self.win[k]
            dst = self.wb[k]
            shp = src.shape
            n0 = shp[0]
            per = int(np.prod(shp[1:]))
            step = max(1, (2 << 20) // per)
            for a in range(0, n0, step):
                b = min(n0, a + step)
                K.dma(K.pool, dst[a:b], src[a:b])
        S = Scope(K)
        valid = S.sb([128, 21 * 128], F32, "cval")
        vb = Buf()
        K.dma(K.sp, valid[:], self.win["c_valid"][:, :], writes=[vb])
        rin = Ring([S.sb([128, 21 * 128], F32, "cb") for _ in range(2)])
        rex = Ring([S.sb([128, 21 * 128], F32, "ce") for _ in range(2)])
        rout = Ring([S.sb([128, 21 * 128], BF16, "co") for _ in range(2)])
        for h in range(16):
            ti, bi = rin.next()
            te, be = rex.next()
            to, bo = rout.next()
            K.dma(K.sp, ti[:], self.win["c_bias"][h], writes=[bi])
            K.op(K.act, lambda: nc.scalar.activation(out=te[:], in_=ti[:], func=AF.Exp), reads=[bi], writes=[be])
            K.op(K.dve, lambda: nc.vector.tensor_tensor(out=to[:], in0=te[:], in1=valid[:], op=ALU.mult), reads=[be, vb], writes=[bo])
            K.dma(K.sp, self.CT[h], to[:], reads=[bo])
        K.barrier()
        S.close()

    def phase_in(self, g, i, T):
        K = self.K
        nc = self.nc
        S = Scope(K)
        src = self.xin[g][i]
        XTv = self.XT[0].rearrange("(c p) t -> p c t", p=128)
        rx = Ring([S.sb([128, D], F32, "xi") for _ in range(6)])
        rp = Ring([S.ps([128, 512], F32, "pt") for _ in range(4)])
        ro = Ring([S.sb([128, KC, 512], F32, "xo") for _ in range(2)])
        for t0 in range(0, T, 512):
            nb = min(4, (T - t0) // 128)
            xs = []
            for tb in range(nb):
                tx, bx = rx.next()
                K.dma(K.sp, tx[:], src[t0 + tb * 128:t0 + (tb + 1) * 128, :], writes=[bx])
                xs.append((tx, bx))
            to, bo = ro.next()
            K.begin(bo)
            for c in range(KC):
                tp, bp = rp.next()
                K.pe.wait(K.deps((), (bp,)))
                for tb in range(nb):
                    tx, bx = xs[tb]
                    K.pe.wait(K.deps((bx, self.cB), ()))
                    ins = nc.tensor.transpose(out=tp[:, tb * 128:(tb + 1) * 128], in_=tx[:, c * 128:(c + 1) * 128], identity=self.ident[:])
                ev = K.pe.mark(ins)
                for tb in range(nb):
                    _merge(xs[tb][1].r, ev)
                bp.w = {ev[0]: ev[1]}
                bp.r = {}
                n = nb * 128
                if c % 2 == 0:
                    K.op(K.act, lambda: nc.scalar.copy(out=to[:, c, 0:n], in_=tp[:, 0:n]), reads=[bp], pw=[bo])
                else:
                    K.op(K.dve, lambda: nc.vector.tensor_copy(out=to[:, c, 0:n], in_=tp[:, 0:n]), reads=[bp], pw=[bo])
            n = nb * 128
            K.dma(K.pool, XTv[:, :, t0:t0 + n], to[:, :, 0:n], reads=[bo])
        K.barrier()
        S.close()

    def norm_rstd(self, S, xt, bx, N, sq, bsq, pss, bpss, sd, bsd, rstd, brs):
        K = self.K
        nc = self.nc
        K.op(K.act, lambda: nc.scalar.activation(out=sq[:, :, 0:N], in_=xt[:, :, 0:N], func=AF.Square), reads=[bx], writes=[bsq])
        K.mm(bpss, pss[:, 0:N], [(self.ones[:], sq[:, c, 0:N], [bsq, self.cB]) for c in range(KC)])
        K.op(K.act, lambda: nc.scalar.activation(out=sd[:, 0:N], in_=pss[:, 0:N], func=AF.Sqrt, bias=self.epsb[:, 0:1], scale=1.0 / D), reads=[bpss, self.cB], writes=[bsd])
        K.op(K.dve, lambda: nc.vector.reciprocal(out=rstd[:, 0:N], in_=sd[:, 0:N]), reads=[bsd], writes=[brs])

    def make_norm_bufs(self, S, n=2):
        K = self.K
        d = {}
        d["sq"] = Ring([S.sb([128, KC, 512], BF16, "sq") for _ in range(1)])
        d["pss"] = Ring([S.ps([128, 512], F32, "pss") for _ in range(1)])
        d["sd"] = Ring([S.sb([128, 512], F32, "sd") for _ in range(1)])
        d["rstd"] = Ring([S.sb([128, 512], F32, "rstd") for _ in range(n)])
        return d

    def normed(self, nb, xt, bx, N, gi, ht, bh):
        K = self.K
        nc = self.nc
        sq, bsq = nb["sq"].next()
        pss, bpss = nb["pss"].next()
        sd, bsd = nb["sd"].next()
        rstd, brs = nb["rstd"].next()
        self.norm_rstd(None, xt, bx, N, sq, bsq, pss, bpss, sd, bsd, rstd, brs)
        K.begin(bh)
        for c in range(KC):
            K.op(K.dve, lambda: nc.vector.scalar_tensor_tensor(out=ht[:, c, 0:N], in0=xt[:, c, 0:N], scalar=self.gains[:, gi, c:c + 1],
                                                                in1=rstd[:, 0:N], op0=ALU.mult, op1=ALU.mult),
                 reads=[bx, brs, self.cB], pw=[bh])
        return rstd, brs

    def load_weights(self, S, src, shape):
        K = self.K
        t = S.sb(shape, BF16, "w")
        b = Buf(t)
        K.dma(K.sp, t[:], src, writes=[b])
        return t, b

    def rope_evac(self, pm, bpm, psw, bpsw, rows, ct, st, btab, c0, N, tmp, btmp, out_ap, bout):
        K = self.K
        nc = self.nc
        r0, r1 = rows
        t1, t2 = tmp
        K.op(K.dve, lambda: nc.vector.tensor_tensor(out=t1[r0:r1, 0:N], in0=pm[r0:r1, 0:N], in1=ct[r0:r1, c0:c0 + N], op=ALU.mult),
             reads=[bpm, btab], writes=[btmp[0]])
        K.op(K.dve, lambda: nc.vector.tensor_tensor(out=t2[r0:r1, 0:N], in0=psw[r0:r1, 0:N], in1=st[r0:r1, c0:c0 + N], op=ALU.mult),
             reads=[bpsw, btab], writes=[btmp[1]])
        K.op(K.pool, lambda: nc.gpsimd.tensor_tensor(out=out_ap, in0=t1[r0:r1, 0:N], in1=t2[r0:r1, 0:N], op=ALU.add),
             reads=[btmp[0], btmp[1]], pw=[bout])

    def proj_common(self, S, T, cur, nh=2):
        K = self.K
        d = {}
        d["XTv"] = self.XT[cur].rearrange("(c p) t -> p c t", p=128)
        d["rx"] = Ring([S.sb([128, KC, 512], F32, "x") for _ in range(2)])
        d["rh"] = Ring([S.sb([128, KC, 512], BF16, "h") for _ in range(nh)])
        d["nb"] = self.make_norm_bufs(S)
        d["pp"] = Ring([S.ps([128, 512], F32, "pp") for _ in range(6)])
        return d

    def phase_proj_a(self, l, j, T, cur):
        K = self.K
        nc = self.nc
        S = Scope(K)
        d = self.proj_common(S, T, cur)
        wst, bw = self.load_weights(S, self.wb["a_st"][j].rearrange("m p k c -> p m k c"), [128, 20, KC, 128])
        wmv, bwm = self.load_weights(S, self.wb["a_mv"][j], [128, KC, 256])
        rtab = Ring([S.sb([128, 2, 512], F32, "tab") for _ in range(2)])
        ropev = self.win["rope"][0:2].rearrange("two p t -> p two t")
        rtmp = [Ring([S.sb([128, 512], F32, "r1") for _ in range(2)]), Ring([S.sb([128, 512], F32, "r2") for _ in range(2)])]
        rq = Ring([S.sb([128, 10, 512], BF16, "qk") for _ in range(2)])
        rv = Ring([S.sb([128, 4, 256], BF16, "v") for _ in range(2)])
        QTv = self.QT[0:1024, :].rearrange("(m p) t -> p m t", p=128)
        KTv = self.KT[0:256, :].rearrange("(m p) t -> p m t", p=128)
        for t0 in range(0, T, 512):
            N = min(512, T - t0)
            xt, bx = d["rx"].next()
            K.dma(K.sp, xt[:, :, 0:N], d["XTv"][:, :, t0:t0 + N], writes=[bx])
            ht, bh = d["rh"].next()
            self.normed(d["nb"], xt, bx, N, l, ht, bh)
            tabt, btab = rtab.next()
            K.dma(K.sp, tabt[:, :, 0:N], ropev[:, :, t0:t0 + N], writes=[btab])
            ctab = tabt[:, 0, :]
            stab = tabt[:, 1, :]
            qk, bqk = rq.next()
            K.begin(bqk)
            for ci in range(10):
                m_main = ci if ci < 8 else 16 + (ci - 8)
                m_sw = 8 + ci if ci < 8 else 18 + (ci - 8)
                pm, bpm = d["pp"].next()
                K.mm(bpm, pm[:, 0:N], [(wst[:, m_main, k, :], ht[:, k, 0:N], [bw, bh]) for k in range(KC)])
                psw, bpsw = d["pp"].next()
                K.mm(bpsw, psw[:, 0:N], [(wst[:, m_sw, k, :], ht[:, k, 0:N], [bw, bh]) for k in range(KC)])
                t1, b1 = rtmp[0].next()
                t2, b2 = rtmp[1].next()
                self.rope_evac(pm, bpm, psw, bpsw, (0, 128), ctab, stab, btab, 0, N, (t1, t2), (b1, b2), qk[:, ci, 0:N], bqk)
            K.dma(K.pool, QTv[:, :, t0:t0 + N], qk[:, 0:8, 0:N], reads=[bqk])
            K.dma(K.pool, KTv[:, :, t0:t0 + N], qk[:, 8:10, 0:N], reads=[bqk])
            vt, bv = rv.next()
            K.begin(bv)
            nbk = N // 128
            for tb in range(nbk):
                pv, bpv = d["pp"].next()
                K.mm(bpv, pv[:, 0:256], [(ht[:, k, tb * 128:(tb + 1) * 128], wmv[:, k, :], [bwm, bh]) for k in range(KC)])
                K.op(K.act, lambda: nc.scalar.copy(out=vt[:, tb, :], in_=pv[:, 0:256]), reads=[bpv], pw=[bv])
            K.dma(K.pool, self.VS[t0:t0 + N, 0:256].rearrange("(b p) c -> p b c", p=128), vt[:, 0:nbk, :], reads=[bv])
        K.barrier()
        S.close()

    def phase_proj_c(self, l, j, T, cur):
        K = self.K
        nc = self.nc
        S = Scope(K)
        d = self.proj_common(S, T, cur)
        wst, bw = self.load_weights(S, self.wb["c_st"][j].rearrange("m p k c -> p m k c"), [128, 16, KC, 128])
        wmv, bwm = self.load_weights(S, self.wb["c_mv"][j], [128, KC, 1024])
        rq = Ring([S.sb([128, 16, 512], BF16, "qk") for _ in range(2)])
        rv = Ring([S.sb([128, 4, 1024], BF16, "v") for _ in range(2)])
        QTv = self.QT[0:1024, :].rearrange("(m p) t -> p m t", p=128)
        KTv = self.KT[0:1024, :].rearrange("(m p) t -> p m t", p=128)
        for t0 in range(0, T, 512):
            N = min(512, T - t0)
            xt, bx = d["rx"].next()
            K.dma(K.sp, xt[:, :, 0:N], d["XTv"][:, :, t0:t0 + N], writes=[bx])
            ht, bh = d["rh"].next()
            self.normed(d["nb"], xt, bx, N, l, ht, bh)
            qk, bqk = rq.next()
            K.begin(bqk)
            for ci in range(16):
                pm, bpm = d["pp"].next()
                K.mm(bpm, pm[:, 0:N], [(wst[:, ci, k, :], ht[:, k, 0:N], [bw, bh]) for k in range(KC)])
                if ci % 2 == 0:
                    K.op(K.act, lambda: nc.scalar.copy(out=qk[:, ci, 0:N], in_=pm[:, 0:N]), reads=[bpm], pw=[bqk])
                else:
                    K.op(K.dve, lambda: nc.vector.tensor_copy(out=qk[:, ci, 0:N], in_=pm[:, 0:N]), reads=[bpm], pw=[bqk])
            K.dma(K.pool, QTv[:, :, t0:t0 + N], qk[:, 0:8, 0:N], reads=[bqk])
            K.dma(K.pool, KTv[:, :, t0:t0 + N], qk[:, 8:16, 0:N], reads=[bqk])
            vt, bv = rv.next()
            K.begin(bv)
            nbk = N // 128
            for tb in range(nbk):
                for n2 in range(2):
                    pv, bpv = d["pp"].next()
                    K.mm(bpv, pv[:, :], [(ht[:, k, tb * 128:(tb + 1) * 128], wmv[:, k, n2 * 512:(n2 + 1) * 512], [bwm, bh]) for k in range(KC)])
                    if n2 == 0:
                        K.op(K.act, lambda: nc.scalar.copy(out=vt[:, tb, 0:512], in_=pv[:, :]), reads=[bpv], pw=[bv])
                    else:
                        K.op(K.dve, lambda: nc.vector.tensor_copy(out=vt[:, tb, 512:1024], in_=pv[:, :]), reads=[bpv], pw=[bv])
            K.dma(K.pool, self.VS[t0:t0 + N, :].rearrange("(b p) c -> p b c", p=128), vt[:, 0:nbk, :], reads=[bv])
        K.barrier()
        S.close()

    def phase_proj_b(self, l, j, T, cur):
        K = self.K
        nc = self.nc
        S = Scope(K)
        d = self.proj_common(S, T, cur, nh=1)
        win_, bwin = self.load_weights(S, self.wb["b_in"][j].rearrange("m p k c -> p m k c"), [128, 7, KC, 128])
        wuq, bwuq = self.load_weights(S, self.wb["b_uq"][j].rearrange("m p k c -> p m k c"), [128, 32, 3, 128])
        wuk, bwuk = self.load_weights(S, self.wb["b_uk"][j].rearrange("m p k c -> p m k c"), [128, 16, 2, 128])
        wuv, bwuv = self.load_weights(S, self.wb["b_uv"][j], [128, 2, 1024])
        rtab = Ring([S.sb([128, 2, 512], F32, "tab") for _ in range(2)])
        ropev = self.win["rope"][2:4].rearrange("two p t -> p two t")
        rtmp = [Ring([S.sb([128, 512], F32, "r1") for _ in range(2)]), Ring([S.sb([128, 512], F32, "r2") for _ in range(2)])]
        cf = S.sb([128, 5, 512], F32, "cf")
        bcf = Buf()
        csq = S.sb([128, 5, 512], BF16, "csq")
        bcsq = Buf()
        rcn = Ring([S.sb([128, 5, 512], BF16, "cn") for _ in range(2)])
        rkr = Ring([S.sb([128, 512], BF16, "kr") for _ in range(2)])
        pss2 = S.ps([128, 512], F32, "pss2")
        bpss2 = Buf()
        sd2 = S.sb([128, 512], F32, "sd2")
        bsd2 = Buf()
        rs2 = S.sb([128, 2, 512], F32, "rs2")
        brs2 = Buf()
        rq = Ring([S.sb([128, 16, 512], BF16, "q") for _ in range(1)])
        rk = Ring([S.sb([128, 16, 512], BF16, "k") for _ in range(1)])
        rv = Ring([S.sb([128, 4, 1024], BF16, "v") for _ in range(1)])
        QTv = self.QT.rearrange("(h r) t -> r h t", r=96)
        KTv = self.KT.rearrange("(h r) t -> r h t", r=96)
        for t0 in range(0, T, 512):
            N = min(512, T - t0)
            xt, bx = d["rx"].next()
            K.dma(K.sp, xt[:, :, 0:N], d["XTv"][:, :, t0:t0 + N], writes=[bx])
            ht, bh = d["rh"].next()
            self.normed(d["nb"], xt, bx, N, l, ht, bh)
            tabt, btab = rtab.next()
            K.dma(K.sp, tabt[64:96, :, 0:N], ropev[64:96, :, t0:t0 + N], writes=[btab])
            ctab = tabt[:, 0, :]
            stab = tabt[:, 1, :]
            K.begin(bcf)
            K.begin(bcsq)
            for ci in range(5):
                pm, bpm = d["pp"].next()
                K.mm(bpm, pm[:, 0:N], [(win_[:, ci, k, :], ht[:, k, 0:N], [bwin, bh]) for k in range(KC)])
                K.op(K.act, lambda: nc.scalar.copy(out=cf[:, ci, 0:N], in_=pm[:, 0:N]), reads=[bpm], pw=[bcf])
                K.op(K.act, lambda: nc.scalar.activation(out=csq[:, ci, 0:N], in_=pm[:, 0:N], func=AF.Square), reads=[bpm], pw=[bcsq])
            cn, bcn = rcn.next()
            K.begin(bcn)
            K.begin(brs2)
            for part, (c0, c1) in enumerate([(0, 3), (3, 5)]):
                K.mm(bpss2, pss2[:, 0:N], [(self.ones[:], csq[:, c, 0:N], [bcsq, self.cB]) for c in range(c0, c1)])
                K.op(K.act, lambda: nc.scalar.activation(out=sd2[:, 0:N], in_=pss2[:, 0:N], func=AF.Sqrt, bias=self.epsb[:, 0:1], scale=1.0 / (128 * (c1 - c0))),
                     reads=[bpss2, self.cB], writes=[bsd2])
                K.op(K.dve, lambda: nc.vector.reciprocal(out=rs2[:, part, 0:N], in_=sd2[:, 0:N]), reads=[bsd2], pw=[brs2])
                for c in range(c0, c1):
                    K.op(K.dve, lambda: nc.vector.scalar_tensor_tensor(out=cn[:, c, 0:N], in0=cf[:, c, 0:N], scalar=self.bgn[:, c:c + 1], in1=rs2[:, part, 0:N],
                                                                        op0=ALU.mult, op1=ALU.mult),
                         reads=[bcf, brs2, self.cB], pw=[bcn])
            pm, bpm = d["pp"].next()
            K.mm(bpm, pm[:, 0:N], [(win_[:, 5, k, :], ht[:, k, 0:N], [bwin, bh]) for k in range(KC)])
            psw, bpsw = d["pp"].next()
            K.mm(bpsw, psw[:, 0:N], [(win_[:, 6, k, :], ht[:, k, 0:N], [bwin, bh]) for k in range(KC)])
            kr, bkr = rkr.next()
            K.begin(bkr)
            t1, b1 = rtmp[0].next()
            t2, b2 = rtmp[1].next()
            self.rope_evac(pm, bpm, psw, bpsw, (64, 96), ctab, stab, btab, 0, N, (t1, t2), (b1, b2), kr[64:96, 0:N], bkr)
            qt, bq = rq.next()
            kt, bk = rk.next()
            K.begin(bq)
            K.begin(bk)
            for h in range(16):
                pm, bpm = d["pp"].next()
                K.mm(bpm, pm[:, 0:N], [(wuq[:, h, k, :], cn[:, k, 0:N], [bwuq, bcn]) for k in range(3)])
                psw, bpsw = d["pp"].next()
                K.mm(bpsw, psw[:, 0:N], [(wuq[:, 16 + h, k, :], cn[:, k, 0:N], [bwuq, bcn]) for k in range(3)])
                K.op(K.act, lambda: nc.scalar.copy(out=qt[0:64, h, 0:N], in_=pm[0:64, 0:N]), reads=[bpm], pw=[bq])
                t1, b1 = rtmp[0].next()
                t2, b2 = rtmp[1].next()
                self.rope_evac(pm, bpm, psw, bpsw, (64, 96), ctab, stab, btab, 0, N, (t1, t2), (b1, b2), qt[64:96, h, 0:N], bq)
                pk, bpk = d["pp"].next()
                K.mm(bpk, pk[:, 0:N], [(wuk[:, h, k, :], cn[:, 3 + k, 0:N], [bwuk, bcn]) for k in range(2)])
                K.op(K.act, lambda: nc.scalar.copy(out=kt[0:64, h, 0:N], in_=pk[0:64, 0:N]), reads=[bpk], pw=[bk])
                K.op(K.pool, lambda: nc.gpsimd.tensor_copy(out=kt[64:96, h, 0:N], in_=kr[64:96, 0:N]), reads=[bkr], pw=[bk])
            K.dma(K.pool, QTv[:, :, t0:t0 + N], qt[0:96, :, 0:N], reads=[bq])
            K.dma(K.pool, KTv[:, :, t0:t0 + N], kt[0:96, :, 0:N], reads=[bk])
            vt, bv = rv.next()
            K.begin(bv)
            nbk = N // 128
            for tb in range(nbk):
                for n2 in range(2):
                    pv, bpv = d["pp"].next()
                    K.mm(bpv, pv[:, :], [(cn[:, 3 + k, tb * 128:(tb + 1) * 128], wuv[:, k, n2 * 512:(n2 + 1) * 512], [bwuv, bcn]) for k in range(2)])
                    if n2 == 0:
                        K.op(K.act, lambda: nc.scalar.copy(out=vt[:, tb, 0:512], in_=pv[:, :]), reads=[bpv], pw=[bv])
                    else:
                        K.op(K.dve, lambda: nc.vector.tensor_copy(out=vt[:, tb, 512:1024], in_=pv[:, :]), reads=[bpv], pw=[bv])
            K.dma(K.pool, self.VS[t0:t0 + N, :].rearrange("(b p) c -> p b c", p=128), vt[:, 0:nbk, :], reads=[bv])
        K.barrier()
        S.close()

    def phase_att(self, kind, j, T):
        K = self.K
        nc = self.nc
        S = Scope(K)
        Tm = self.Tmax
        nblk = T // 128
        dk = 96 if kind == "b" else 64
        scale = (96.0 ** -0.5) if kind == "b" else 0.125
        SWc = {"a": 512, "b": 1024, "c": 1024}[kind]
        NS = {"a": 4, "b": 3, "c": 3}[kind]
        DEPTH = NS - 1
        NO = 2 if kind == "a" else 1
        sps = S.ps([128, NS * SWc], F32, "S")
        rS = Ring([sps[:, i * SWc:(i + 1) * SWc] for i in range(NS)])
        rO = [Ring([S.ps([128, 512], F32, "O") for _ in range(NO)]) for _ in range(2)]
        rQ = [Ring([S.sb([128, Tm], BF16, "Q") for _ in range(2)]) for _ in range(2)]
        rK = [Ring([S.sb([128, Tm], BF16, "Kt") for _ in range(2)]) for _ in range(2)]
        vx_t = [S.sb([128, Tm // 128, 192], BF16, "VX") for _ in range(2)]
        rVX = Ring(vx_t)
        for i, t in enumerate(vx_t):
            K.op(K.pool, lambda: nc.gpsimd.memset(t[:, :, 64:128], 1.0), writes=[rVX.b[i]])
        tab = None
        btab = Buf()
        if kind == "a":
            tab = S.sb([128, 384], BF16, "am")
            K.dma(K.sp, tab[:], self.wb["a_mask"][:, :], writes=[btab])
            sk = S.sb([128, 16], F32, "sink")
            ske = S.sb([128, 16], F32, "sinke")
            bsk = Buf()
            K.dma(K.sp, sk[:], self.win["a_sink"][j].partition_broadcast(128), writes=[bsk])
            K.op(K.act, lambda: nc.scalar.activation(out=ske[:], in_=sk[:], func=AF.Exp), reads=[bsk], writes=[bsk])
        if kind == "c":
            rtab = Ring([S.sb([128, 2, 21 * 128], BF16, "ctab") for _ in range(2)])
            types = c_block_types(T // 64)
        rE = Ring([S.sb([128, SWc], BF16, "E") for _ in range(NS + 1)])
        rR = Ring([S.sb([128, 512], F32, "R") for _ in range(2)])
        rOT = Ring([S.sb([128, 512], BF16, "OTt") for _ in range(2)])
        OTv = self.OT.rearrange("(m p) t -> p m t", p=128)
        pend = []

        def front(it):
            ts_, bs = rS.next()
            tq, bq = it["Q"]
            tk, bk = it["K"]
            K.pe.wait(K.deps((bq, bk), (bs,)))
            ins = None
            for (c0, n, kb, qc0) in it["smm"]:
                ins = nc.tensor.matmul(ts_[:, c0:c0 + n], lhsT=tk[0:dk, kb * 128:(kb + 1) * 128], rhs=tq[0:dk, qc0:qc0 + n],
                                       start=True, stop=True, skip_group_check=True)
            ev = K.pe.mark(ins)
            _merge(bq.r, ev)
            _merge(bk.r, ev)
            bs.w = {ev[0]: ev[1]}
            bs.r = {}
            te, be = rE.next()
            K.begin(be)
            for (c0, n) in it["exp"]:
                K.op(K.act, lambda: nc.scalar.activation(out=te[:, c0:c0 + n], in_=ts_[:, c0:c0 + n], func=AF.Exp, scale=scale), reads=[bs], pw=[be])
            if it["tab"] is not None:
                tap, tb_, n = it["tab"]
                K.op(K.dve, lambda: nc.vector.tensor_tensor(out=te[:, 0:n], in0=te[:, 0:n], in1=tap, op=ALU.mult), reads=[tb_], writes=[be])
            it["E"] = (te, be)

        def back(it):
            te, be = it["E"]
            po, bpo = it["O"]
            vx, bvx = it["V"]
            if it["ofirst"]:
                K.pe.wait(K.deps((), (bpo,)))
            K.pe.wait(K.deps((be, bvx), ()))
            ins = None
            for (oc0, n, kb, ec0, st, sp_) in it["pv"]:
                ins = nc.tensor.matmul(po[:, oc0:oc0 + n], lhsT=vx[:, kb, it["win"]], rhs=te[:, ec0:ec0 + n], start=st, stop=sp_, skip_group_check=True)
            ev = K.pe.mark(ins)
            _merge(be.r, ev)
            _merge(bvx.r, ev)
            if it["olast"]:
                bpo.w = {ev[0]: ev[1]}
                bpo.r = {}
            if it["post"] is not None:
                it["post"]()

        def push(it):
            front(it)
            pend.append(it)
            if len(pend) > DEPTH:
                back(pend.pop(0))

        for hp in range(8):
            hs = (2 * hp, 2 * hp + 1)
            Qs, Ks = [], []
            for a in range(2):
                h = hs[a]
                tq, bq = rQ[a].next()
                tk, bk = rK[a].next()
                K.dma(K.sp, tq[0:dk, 0:T], self.QT[dk * h:dk * h + dk, 0:T], writes=[bq])
                kvh = h // 4 if kind == "a" else h
                K.dma(K.sp, tk[0:dk, 0:T], self.KT[dk * kvh:dk * kvh + dk, 0:T], writes=[bk])
                Qs.append((tq, bq))
                Ks.append((tk, bk))
            vx, bvx = rVX.next()
            K.begin(bvx)
            for a in range(2):
                h = hs[a]
                kvh = h // 4 if kind == "a" else h
                K.dma(K.sp, vx[:, 0:nblk, 128 * a:128 * a + 64], self.VS[0:T, 64 * kvh:64 * kvh + 64].rearrange("(b p) c -> p b c", p=128), pw=[bvx])
            if kind == "c":
                ctb, bct = rtab.next()
                K.dma(K.sp, ctb[:, :, :], self.CT[2 * hp:2 * hp + 2].rearrange("h p c -> p h c"), writes=[bct])
            for q0 in range(0, T, 512):
                nq = min(512, T - q0)
                Ot = [rO[a].next() for a in range(2)]

                def post(Ot=Ot, q0=q0, nq=nq, hp=hp, hs=hs):
                    tr, br = rR.next()
                    to, bo = rOT.next()
                    (poA, bA), (poB, bB) = Ot
                    hA, hB = hs
                    K.begin(br)
                    if kind == "a":
                        K.op(K.dve, lambda: nc.vector.tensor_scalar(out=tr[0:64, 0:nq], in0=poA[64:128, 0:nq], scalar1=ske[64:128, hA:hA + 1], scalar2=None, op0=ALU.add),
                             reads=[bA, bsk], pw=[br])
                        K.op(K.dve, lambda: nc.vector.tensor_scalar(out=tr[64:128, 0:nq], in0=poB[0:64, 0:nq], scalar1=ske[0:64, hB:hB + 1], scalar2=None, op0=ALU.add),
                             reads=[bB, bsk], pw=[br])
                        K.op(K.dve, lambda: nc.vector.reciprocal(out=tr[:, 0:nq], in_=tr[:, 0:nq]), reads=[br], writes=[br])
                    else:
                        K.op(K.dve, lambda: nc.vector.reciprocal(out=tr[0:64, 0:nq], in_=poA[64:128, 0:nq]), reads=[bA], pw=[br])
                        K.op(K.dve, lambda: nc.vector.reciprocal(out=tr[64:128, 0:nq], in_=poB[0:64, 0:nq]), reads=[bB], pw=[br])
                    K.begin(bo)
                    K.op(K.dve, lambda: nc.vector.tensor_tensor(out=to[0:64, 0:nq], in0=poA[0:64, 0:nq], in1=tr[0:64, 0:nq], op=ALU.mult), reads=[bA, br], pw=[bo])
                    K.op(K.dve, lambda: nc.vector.tensor_tensor(out=to[64:128, 0:nq], in0=poB[64:128, 0:nq], in1=tr[64:128, 0:nq], op=ALU.mult), reads=[bB, br], pw=[bo])
                    K.dma(K.pool, OTv[:, hp, q0:q0 + nq], to[:, 0:nq], reads=[bo])

                items = []
                if kind == "b":
                    npair = nblk // 2
                    for a in range(2):
                        for kp in range(npair):
                            items.append(dict(
                                Q=Qs[a], K=Ks[a], V=(vx, bvx), O=Ot[a], win=(slice(0, 128) if a == 0 else slice(64, 192)),
                                smm=[(i2 * 512, nq, 2 * kp + i2, q0) for i2 in range(2)],
                                exp=[(0, 1024)] if nq == 512 else [(0, nq), (512, nq)],
                                tab=None,
                                pv=[(0, nq, 2 * kp + i2, i2 * 512, (kp == 0 and i2 == 0), (kp == npair - 1 and i2 == 1)) for i2 in range(2)],
                                ofirst=(kp == 0), olast=(kp == npair - 1), post=None))
                else:
                    nqb = nq // 128
                    for qb in range(nqb):
                        n = (q0 // 128) + qb
                        if kind == "a":
                            kbs = [kb for kb in (n - 1, n, n + 1) if 0 <= kb < nblk]
                            tc0 = (kbs[0] - (n - 1)) * 128
                        else:
                            ty, kb0, nk_ = types[n]
                            kbs = list(range(kb0, kb0 + nk_))
                            tc0 = C_TOFF[ty] * 128
                        nk = len(kbs)
                        for a in range(2):
                            if kind == "a":
                                tb3 = (tab[:, tc0:tc0 + nk * 128], btab, nk * 128)
                            else:
                                tb3 = (ctb[:, a, tc0:tc0 + nk * 128], bct, nk * 128)
                            items.append(dict(
                                Q=Qs[a], K=Ks[a], V=(vx, bvx), O=Ot[a], win=(slice(0, 128) if a == 0 else slice(64, 192)),
                                smm=[(i * 128, 128, kb, q0 + qb * 128) for i, kb in enumerate(kbs)],
                                exp=[(0, nk * 128)],
                                tab=tb3,
                                pv=[(qb * 128, 128, kb, i * 128, (i == 0), (i == nk - 1)) for i, kb in enumerate(kbs)],
                                ofirst=(qb == 0), olast=(qb == nqb - 1), post=None))
                items[-1]["post"] = post
                for it in items:
                    push(it)
        while pend:
            back(pend.pop(0))
        K.barrier()
        S.close()

    def phase_woffn(self, l, wo_src, T, cur):
        K = self.K
        nc = self.nc
        S = Scope(K)
        XTi = self.XT[cur].rearrange("(c p) t -> p c t", p=128)
        XTo = self.XT[1 - cur].rearrange("(c p) t -> p c t", p=128)
        OTv = self.OT.rearrange("(m p) t -> p m t", p=128)
        wo, bwo = self.load_weights(S, wo_src.rearrange("m p k c -> p m k c"), [128, KC, KC, 128])
        cw = S.sb([128, 44, 4], F32, "cw")
        bcw = Buf()
        K.dma(K.sp, cw[:], self.win["f_cw"][l], writes=[bcw])
        rx = Ring([S.sb([128, KC, 512], F32, "x") for _ in range(2)])
        ro = Ring([S.sb([128, KC, 512], BF16, "o") for _ in range(2)])
        rh = Ring([S.sb([128, KC, 512], BF16, "h") for _ in range(2)])
        nb = self.make_norm_bufs(S)
        rwin = Ring([S.sb([128, 2, KC, 128], BF16, "wi") for _ in range(4)])
        rwout = Ring([S.sb([128, NF, 128], BF16, "wo") for _ in range(2)])
        pp = Ring([S.ps([128, 512], F32, "pp") for _ in range(4)])
        pacc = Ring([S.ps([128, 512], F32, "pa") for _ in range(2)])
        rc = [Ring([S.sb([128, 512], F32, "c%d" % i) for _ in range(2)]) for i in range(3)]
        rG = Ring([S.sb([128, NF, 512], BF16, "G") for _ in range(2)])
        fin2 = self.wb["f_in"][l].rearrange("(g f) p k c -> g f p k c", g=2)
        fout = self.wb["f_out"][l]
        for (o0, no) in ffn_windows(T):
            a0 = max(o0 - 1, 0)
            a1 = min(o0 + no + 1, T)
            N = no + 2
            cs = a0 - (o0 - 1)
            ce = cs + (a1 - a0)
            xt, bx = rx.next()
            ot, bo = ro.next()
            ht, bh = rh.next()
            K.dma(K.sp, xt[:, :, cs:ce], XTi[:, :, a0:a1], writes=[bx])
            K.dma(K.sp, ot[:, :, cs:ce], OTv[:, :, a0:a1], writes=[bo])
            for m in range(KC):
                pm, bpm = pp.next()
                K.mm(bpm, pm[:, cs:ce], [(wo[:, m, k, :], ot[:, k, cs:ce], [bwo, bo]) for k in range(KC)])
                K.op(K.dve, lambda: nc.vector.tensor_tensor(out=xt[:, m, cs:ce], in0=pm[:, cs:ce], in1=xt[:, m, cs:ce], op=ALU.add), reads=[bpm], writes=[bx])
            if cs > 0:
                K.op(K.pool, lambda: nc.gpsimd.memset(xt[:, :, 0:1], 0.0), writes=[bx])
            if ce < N:
                K.op(K.pool, lambda: nc.gpsimd.memset(xt[:, :, N - 1:N], 0.0), writes=[bx])
            self.normed(nb, xt, bx, N, 4 + l, ht, bh)
            Gt, bG = rG.next()
            K.begin(bG)
            for f in range(NF):
                wt, bwt = rwin.next()
                K.dma(K.sp, wt[:], fin2[:, f].rearrange("g p k c -> p g k c"), writes=[bwt])
                res = []
                for half in range(2):
                    fc = f + NF * half
                    pu, bpu = pp.next()
                    K.mm(bpu, pu[:, 0:N], [(wt[:, half, k, :], ht[:, k, 0:N], [bwt, bh]) for k in range(KC)])
                    ct_, bc = rc[half].next()
                    K.op(K.act, lambda: nc.scalar.activation(out=ct_[:, 0:no], in_=pu[:, 1:1 + no], func=AF.Identity, bias=cw[:, fc, 3:4], scale=cw[:, fc, 1:2]),
                         reads=[bpu, bcw], writes=[bc])
                    K.op(K.dve, lambda: nc.vector.scalar_tensor_tensor(out=ct_[:, 0:no], in0=pu[:, 0:no], scalar=cw[:, fc, 0:1], in1=ct_[:, 0:no], op0=ALU.mult, op1=ALU.add),
                         reads=[bpu, bcw], writes=[bc])
                    K.op(K.dve, lambda: nc.vector.scalar_tensor_tensor(out=ct_[:, 0:no], in0=pu[:, 2:2 + no], scalar=cw[:, fc, 2:3], in1=ct_[:, 0:no], op0=ALU.mult, op1=ALU.add),
                         reads=[bpu, bcw], writes=[bc])
                    res.append((ct_, bc))
                sg, bsg = rc[2].next()
                K.op(K.act, lambda: nc.scalar.activation(out=sg[:, 0:no], in_=res[0][0][:, 0:no], func=AF.Silu), reads=[res[0][1]], writes=[bsg])
                K.op(K.pool, lambda: nc.gpsimd.tensor_tensor(out=Gt[:, f, 0:no], in0=sg[:, 0:no], in1=res[1][0][:, 0:no], op=ALU.mult),
                     reads=[bsg, res[1][1]], pw=[bG])
            for m in range(KC):
                wt, bwt = rwout.next()
                K.dma(K.sp, wt[:], fout[m], writes=[bwt])
                pa, bpa = pacc.next()
                K.mm(bpa, pa[:, 0:no], [(wt[:, f, :], Gt[:, f, 0:no], [bwt, bG]) for f in range(NF)])
                K.op(K.dve, lambda: nc.vector.tensor_tensor(out=xt[:, m, 1:1 + no], in0=pa[:, 0:no], in1=xt[:, m, 1:1 + no], op=ALU.add), reads=[bpa], writes=[bx])
            K.dma(K.pool, XTo[:, :, o0:o0 + no], xt[:, :, 1:1 + no], reads=[bx])
        K.barrier()
        S.close()

    def phase_final(self, g, i, T, cur):
        K = self.K
        nc = self.nc
        S = Scope(K)
        XTv = self.XT[cur].rearrange("(c p) t -> p c t", p=128)
        dst = self.yout[g][i]
        rx = Ring([S.sb([128, KC, 512], F32, "x") for _ in range(2)])
        ry = Ring([S.sb([128, KC, 512], F32, "y") for _ in range(2)])
        nb = self.make_norm_bufs(S)
        rp = Ring([S.ps([128, 512], F32, "pt") for _ in range(4)])
        rout = Ring([S.sb([128, D], F32, "yo") for _ in range(4)])
        for t0 in range(0, T, 512):
            N = min(512, T - t0)
            xt, bx = rx.next()
            K.dma(K.sp, xt[:, :, 0:N], XTv[:, :, t0:t0 + N], writes=[bx])
            sq, bsq = nb["sq"].next()
            pss, bpss = nb["pss"].next()
            sd, bsd = nb["sd"].next()
            rstd, brs = nb["rstd"].next()
            self.norm_rstd(None, xt, bx, N, sq, bsq, pss, bpss, sd, bsd, rstd, brs)
            yt, by = ry.next()
            K.begin(by)
            for c in range(KC):
                K.op(K.dve, lambda: nc.vector.scalar_tensor_tensor(out=yt[:, c, 0:N], in0=xt[:, c, 0:N], scalar=self.gains[:, 8, c:c + 1], in1=rstd[:, 0:N], op0=ALU.mult, op1=ALU.mult),
                     reads=[bx, brs, self.cB], pw=[by])
            for tb in range(N // 128):
                to, bo = rout.next()
                K.begin(bo)
                for half in range(2):
                    tp, bp = rp.next()
                    K.pe.wait(K.deps((by, self.cB), (bp,)))
                    for c4 in range(4):
                        c = half * 4 + c4
                        ins = nc.tensor.transpose(out=tp[:, c4 * 128:(c4 + 1) * 128], in_=yt[:, c, tb * 128:(tb + 1) * 128], identity=self.ident[:])
                    ev = K.pe.mark(ins)
                    _merge(by.r, ev)
                    bp.w = {ev[0]: ev[1]}
                    bp.r = {}
                    if half == 0:
                        K.op(K.act, lambda: nc.scalar.copy(out=to[:, 0:512], in_=tp[:, :]), reads=[bp], pw=[bo])
                    else:
                        K.op(K.dve, lambda: nc.vector.tensor_copy(out=to[:, 512:1024], in_=tp[:, :]), reads=[bp], pw=[bo])
                K.dma(K.pool, dst[t0 + tb * 128:t0 + (tb + 1) * 128, :], to[:], reads=[bo])
        K.barrier()
        S.close()


_CACHE = {}


def kernel(**inputs):
    xp = np.asarray(inputs["x_prompt"], dtype=np.float32)
    xs = np.asarray(inputs["x_sample"], dtype=np.float32)
    sh = prep_shared(inputs)
    npc = xp.shape[0] // NCORES
    nsc = xs.shape[0] // NCORES
    seqs = [("p", i, xp.shape[1]) for i in range(npc)] + [("s", i, xs.shape[1]) for i in range(nsc)]
    prog = Prog(seqs, {k: v.shape for k, v in sh.items()})
    in_maps = []
    for c in range(NCORES):
        m = {"x_p": np.ascontiguousarray(xp[c * npc:(c + 1) * npc]), "x_s": np.ascontiguousarray(xs[c * nsc:(c + 1) * nsc])}
        for k, v in sh.items():
            m["w_" + k] = v
        in_maps.append(m)
    res = run_bass_kernel_spmd(prog.nc, in_maps, core_ids=list(range(NCORES)))
    yp = np.concatenate([r["y_p"] for r in res.results], 0)
    ys = np.concatenate([r["y_s"] for r in res.results], 0)
    return (yp.astype(np.float32), ys.astype(np.float32))
```
